# Optimizing a Trainium2 kernel written in Bass

```python
import jax, jax.numpy as jnp
from jax import lax
import numpy as np

D_MODEL = 2048
BATCH = 2
SEQ = 8192
DEPTH = 1

RET_HEADS = 4
RET_HEAD_DIM = D_MODEL // 8
RET_WIDTH = RET_HEADS * RET_HEAD_DIM
HGRN_HEADS = 8
HGRN_HEAD_DIM = D_MODEL // 16
HGRN_WIDTH = HGRN_HEADS * HGRN_HEAD_DIM
MIX_WIDTH = RET_WIDTH + HGRN_WIDTH
IN_COLS = 4 * RET_WIDTH + 4 * HGRN_WIDTH
D_FF = ((8 * D_MODEL // 3 + 255) // 256) * 256
RET_CHUNK = 128
HGRN_CHUNK = 64
ROPE_BASE = 10000.0
EPS = 1e-6
FFN_RESIDUAL_WEIGHT = 0.5

kernel_name = "hymba_style_retention_hgrn2_macaron"


def rmsnorm(x, g):
    x32 = x.astype(jnp.float32)
    y = x32 * lax.rsqrt(jnp.mean(x32 * x32, axis=-1, keepdims=True) + EPS)
    return (y * g.astype(jnp.float32)).astype(x.dtype)


def swiglu(h, w_gate, w_up, w_down):
    return (jax.nn.silu(h @ w_gate) * (h @ w_up)) @ w_down


def rope(x):
    d = x.shape[-1]
    s = x.shape[1]
    inv = jnp.power(ROPE_BASE, -jnp.arange(0, d, 2, dtype=jnp.float32) / d)
    ang = jnp.arange(s, dtype=jnp.float32)[:, None] * inv[None, :]
    cos = jnp.cos(ang)[None, :, None, :]
    sin = jnp.sin(ang)[None, :, None, :]
    x32 = x.astype(jnp.float32)
    x1, x2 = x32[..., : d // 2], x32[..., d // 2:]
    return jnp.concatenate([x1 * cos - x2 * sin, x2 * cos + x1 * sin], axis=-1)


def retention_chunkwise(q, k, v):
    b, s, h, dk = q.shape
    dv = v.shape[-1]
    c = RET_CHUNK
    n = s // c
    log_gamma = jnp.log(1.0 - jnp.exp2(-5.0 - jnp.arange(h, dtype=jnp.float32)))
    q = q.astype(jnp.float32).reshape(b, n, c, h, dk)
    k = k.astype(jnp.float32).reshape(b, n, c, h, dk)
    v = v.astype(jnp.float32).reshape(b, n, c, h, dv)
    idx = jnp.arange(c, dtype=jnp.float32)
    rel = idx[:, None] - idx[None, :]
    mask = rel >= 0
    decay = jnp.where(mask[None], jnp.exp(log_gamma[:, None, None] * jnp.where(mask, rel, 0.0)[None]), 0.0)
    scores = jnp.einsum('bnihd,bnjhd->bnhij', q, k) * decay[None, None]
    inner = jnp.einsum('bnhij,bnjhe->bnihe', scores, v)
    k_dec = k * jnp.exp(log_gamma[None, :] * (c - 1.0 - idx)[:, None])[None, None, :, :, None]
    kv = jnp.einsum('bnjhd,bnjhe->nbhde', k_dec, v)
    g_chunk = jnp.exp(log_gamma * c)[None, :, None, None]

    def step(state, kv_n):
        return g_chunk * state + kv_n, state

    _, r_prev = lax.scan(step, jnp.zeros((b, h, dk, dv), jnp.float32), kv)
    q_dec = q * jnp.exp(log_gamma[None, :] * (idx + 1.0)[:, None])[None, None, :, :, None]
    cross = jnp.einsum('bnihd,nbhde->bnihe', q_dec, r_prev)
    return (inner + cross).reshape(b, s, h, dv)


def hgrn2_chunkwise(q, k, v, log_f):
    b, s, h, dk = q.shape
    dv = v.shape[-1]
    c = HGRN_CHUNK
    n = s // c

    def to_chunks(t):
        return t.astype(jnp.float32).reshape(b, n, c, h, t.shape[-1]).transpose(1, 0, 3, 2, 4)

    causal = jnp.tril(jnp.ones((c, c), dtype=bool))

    def step(state, inp):
        q_c, k_c, v_c, lf_c = inp
        cum = jnp.cumsum(lf_c, axis=-2)
        diff = cum[:, :, :, None, :] - cum[:, :, None, :, :]
        pair_decay = jnp.exp(jnp.where(causal[:, :, None], diff, -jnp.inf))
        attn = jnp.einsum('bhtd,bhjd,bhtjd->bhtj', q_c, k_c, pair_decay)
        o = jnp.einsum('bhtj,bhje->bhte', attn, v_c) + jnp.einsum('bhtd,bhde->bhte', q_c * jnp.exp(cum), state)
        last = cum[:, :, -1:, :]
        state = jnp.exp(last[:, :, 0, :])[..., None] * state + jnp.einsum('bhjd,bhje->bhde', k_c * jnp.exp(last - cum), v_c)
        return state, o

    _, o = lax.scan(step, jnp.zeros((b, h, dk, dv), jnp.float32),
                    (to_chunks(q), to_chunks(k), to_chunks(v), to_chunks(log_f)))
    return o.transpose(1, 0, 3, 2, 4).reshape(b, s, h, dv)


def hgrn_lower_bounds(lb_logits):
    logits = jnp.concatenate([lb_logits.astype(jnp.float32), jnp.zeros((1, lb_logits.shape[-1]), jnp.float32)], axis=0)
    return jnp.cumsum(jax.nn.softmax(logits, axis=0), axis=0)[:DEPTH]


def hybrid_mixer(h, w_in, ret_norm_g, lb, hgrn_norm_g, w_out):
    b, s, _ = h.shape
    proj = h @ w_in
    splits = np.cumsum([RET_WIDTH] * 4 + [HGRN_WIDTH] * 3).tolist()
    rq, rk, rv, rg, hq, hf, hi, hg = jnp.split(proj, splits, axis=-1)
    rq = rope(rq.reshape(b, s, RET_HEADS, RET_HEAD_DIM)) * (RET_HEAD_DIM ** -0.5)
    rk = rope(rk.reshape(b, s, RET_HEADS, RET_HEAD_DIM))
    ret = retention_chunkwise(rq, rk, rv.reshape(b, s, RET_HEADS, RET_HEAD_DIM))
    mu = jnp.mean(ret, axis=-1, keepdims=True)
    var = jnp.mean(jnp.square(ret - mu), axis=-1, keepdims=True)
    ret = ((ret - mu) * lax.rsqrt(var + EPS)).reshape(b, s, RET_WIDTH)
    ret = ret * ret_norm_g.astype(jnp.float32) * jax.nn.silu(rg.astype(jnp.float32))
    z = hf.astype(jnp.float32).reshape(b, s, HGRN_HEADS, HGRN_HEAD_DIM)
    lbh = lb.reshape(HGRN_HEADS, HGRN_HEAD_DIM)
    f = lbh + (1.0 - lbh) * jax.nn.sigmoid(z)
    key = (1.0 - lbh) * jax.nn.sigmoid(-z)
    hq_act = jax.nn.silu(hq.astype(jnp.float32)).reshape(b, s, HGRN_HEADS, HGRN_HEAD_DIM)
    hv = hi.reshape(b, s, HGRN_HEADS, HGRN_HEAD_DIM)
    hg_out = hgrn2_chunkwise(hq_act, key, hv, jnp.log(f))
    hg_out = (hg_out * lax.rsqrt(jnp.mean(hg_out * hg_out, axis=-1, keepdims=True) + EPS)).reshape(b, s, HGRN_WIDTH)
    hg_out = hg_out * hgrn_norm_g.astype(jnp.float32) * jax.nn.silu(hg.astype(jnp.float32))
    merged = jnp.concatenate([ret, hg_out], axis=-1).astype(h.dtype)
    return merged @ w_out


def setup_inputs(seed: int = 0) -> dict:
    key = jax.random.key(seed)
    ks = jax.random.split(key, 16)
    f32 = jnp.float32

    def w(k, shape, fan_in):
        return jax.random.normal(k, shape, f32) * (fan_in ** -0.5)

    def gain(k, shape):
        return 1.0 + 0.02 * jax.random.normal(k, shape, f32)

    return {
        "x": jax.random.normal(ks[0], (BATCH, SEQ, D_MODEL), f32),
        "ffn1_norm": gain(ks[1], (DEPTH, D_MODEL)),
        "ffn1_w_gate": w(ks[2], (DEPTH, D_MODEL, D_FF), D_MODEL),
        "ffn1_w_up": w(ks[3], (DEPTH, D_MODEL, D_FF), D_MODEL),
        "ffn1_w_down": w(ks[4], (DEPTH, D_FF, D_MODEL), D_FF),
        "mix_norm": gain(ks[5], (DEPTH, D_MODEL)),
        "w_in": w(ks[6], (DEPTH, D_MODEL, IN_COLS), D_MODEL),
        "ret_norm_g": gain(ks[7], (DEPTH, RET_WIDTH)),
        "hgrn_lb_logits": 0.5 * jax.random.normal(ks[8], (DEPTH, HGRN_WIDTH), f32),
        "hgrn_norm_g": gain(ks[9], (DEPTH, HGRN_WIDTH)),
        "w_out": w(ks[10], (DEPTH, MIX_WIDTH, D_MODEL), MIX_WIDTH),
        "ffn2_norm": gain(ks[11], (DEPTH, D_MODEL)),
        "ffn2_w_gate": w(ks[12], (DEPTH, D_MODEL, D_FF), D_MODEL),
        "ffn2_w_up": w(ks[13], (DEPTH, D_MODEL, D_FF), D_MODEL),
        "ffn2_w_down": w(ks[14], (DEPTH, D_FF, D_MODEL), D_FF),
        "final_norm": gain(ks[15], (D_MODEL,)),
    }


def reference(x, ffn1_norm, ffn1_w_gate, ffn1_w_up, ffn1_w_down, mix_norm, w_in, ret_norm_g,
              hgrn_lb_logits, hgrn_norm_g, w_out, ffn2_norm, ffn2_w_gate, ffn2_w_up, ffn2_w_down, final_norm):
    lbs = hgrn_lower_bounds(hgrn_lb_logits)
    for l in range(DEPTH):
        y = swiglu(rmsnorm(x, ffn1_norm[l]), ffn1_w_gate[l], ffn1_w_up[l], ffn1_w_down[l])
        x = x + (FFN_RESIDUAL_WEIGHT * y).astype(x.dtype)
        y = hybrid_mixer(rmsnorm(x, mix_norm[l]), w_in[l], ret_norm_g[l], lbs[l], hgrn_norm_g[l], w_out[l])
        x = x + y.astype(x.dtype)
        y = swiglu(rmsnorm(x, ffn2_norm[l]), ffn2_w_gate[l], ffn2_w_up[l], ffn2_w_down[l])
        x = x + (FFN_RESIDUAL_WEIGHT * y).astype(x.dtype)
    return rmsnorm(x, final_norm)
```

```python
import numpy as np
import concourse.bass as bass
import concourse.mybir as mybir
from concourse.bass_utils import run_bass_kernel_spmd

F32 = mybir.dt.float32
F32R = mybir.dt.float32r
BF16 = mybir.dt.bfloat16
AF = mybir.ActivationFunctionType
ALU = mybir.AluOpType

P = 128
D = 2048
DC = 16
FF = 5632
FC = 44
FH = 22
T = 512
NT = 4
SEQ = 8192
SEG = 2048
N_PRE = 12
N_OWN = 4
NB = N_PRE + N_OWN
EPS = 1e-6
DBG = 9
GAM = [1.0 - 2.0 ** (-5 - h) for h in range(4)]


class Buf:
    def __init__(self, t, name):
        self.t = t
        self.name = name
        self.w = set()
        self.r = set()


class Sched:
    ENGS = ["pe", "act", "dve", "pool", "sp"]

    def __init__(self):
        self.streams = {e: [] for e in self.ENGS}
        self.count = {e: 0 for e in self.ENGS}
        self.waited = {e: {} for e in self.ENGS}
        self.dcount = {}

    def _waits(self, eng, deps):
        need = {}
        for (s, v) in deps:
            if s == eng and eng == "pe":
                continue
            if v > need.get(s, 0):
                need[s] = v
        out = []
        wd = self.waited[eng]
        for s, v in need.items():
            if wd.get(s, 0) < v:
                wd[s] = v
                out.append((s, v))
        return out

    def op(self, eng, fn, reads=(), writes=()):
        deps = set()
        for b in reads:
            deps |= b.w
        for b in writes:
            deps |= b.w
            deps |= b.r
        waits = self._waits(eng, deps)
        self.count[eng] += 1
        tok = (eng, self.count[eng])
        self.streams[eng].append((waits, fn, (eng, 1)))
        for b in writes:
            b.w = {tok}
            b.r = set()
        for b in reads:
            if b not in writes:
                b.r.add(tok)

    def dma(self, eng, sem, fn, reads=(), writes=()):
        deps = set()
        for b in reads:
            deps |= b.w
        for b in writes:
            deps |= b.w
            deps |= b.r
        waits = self._waits(eng, deps)
        self.dcount[sem] = self.dcount.get(sem, 0) + 16
        tok = (sem, self.dcount[sem])
        self.streams[eng].append((waits, fn, (sem, 16)))
        for b in writes:
            b.w = {tok}
            b.r = set()
        for b in reads:
            if b not in writes:
                b.r.add(tok)
        return tok

    def wait(self, eng, toks):
        waits = self._waits(eng, set(toks))
        self.streams[eng].append((waits, None, None))


def build_nc(N_PRE=N_PRE, N_OWN=N_OWN):
    NB = N_PRE + N_OWN
    nc = bass.Bass("TRN2", target_bir_lowering=False)
    S = Sched()

    def dram(name, shape, kind="ExternalInput", dt=F32):
        return nc.dram_tensor(name, list(shape), dt, kind=kind).ap()

    x_all = dram("x_all", [NB * T, D])
    cs_all = dram("cs_all", [NB, P, 2 * T])
    wgu = [dram("wgu1", [FC, P, 4096]), dram("wgu2", [FC, P, 4096])]
    wdn = [dram("wd1", [2 * DC, P, FH * P]), dram("wd2", [2 * DC, P, FH * P])]
    wfm = dram("wfm", [16, P, 4096])
    wtm = dram("wtm", [16, P, 4096])
    wout = dram("wout", [8, P, 4096])
    cst = dram("cst", [P, 1800])
    cols = dram("cols", [P, 128])
    out = dram("out", [N_OWN * T, D], kind="ExternalOutput")

    import contextlib
    es = contextlib.ExitStack()

    def sb(name, shape, dt):
        return es.enter_context(nc.sbuf_tensor(name, list(shape), dt))

    def ps(name, shape, dt):
        return es.enter_context(nc.psum_tensor(name, list(shape), dt))

    with es:
        xT_t = sb("xT", [P, DC, T], F32)
        hT_t = sb("hT", [P, DC, T], BF16)
        mT_t = sb("mT", [P, DC, T], BF16)
        a_t = sb("a", [P, FH, T], BF16)
        xin_t = sb("xin", [P, D], F32)
        wsl_t = [sb(f"ws{i}", [P, 4096], BF16) for i in range(4)]
        cs_t = sb("cs", [P, 2 * T], F32)
        cst_t = sb("cstt", [P, 1800], F32)
        cols_t = sb("colst", [P, 128], F32)
        lbc_t = sb("lbc", [P, 32], F32)
        idb_t = sb("idb", [P, P], BF16)
        rs_t = sb("rs", [P, T], F32)
        sqr_t = [sb(f"sqr{i}", [P, T], F32R) for i in range(2)]
        onesr_t = sb("onesr", [P, P], F32R)
        tmp_t = [sb(f"tmp{i}", [P, T], F32) for i in range(6)]
        qT_t = sb("qT", [P, 2, T], BF16)
        kT_t = sb("kT", [P, 2, T], BF16)
        qd_t = sb("qd", [P, 2, T], BF16)
        v_t = sb("v", [P, NT, 256], BF16)
        gt_t = sb("gt", [P, NT, 256], BF16)
        kd_t = sb("kd", [P, NT, 256], BF16)
        pt_t = sb("pt", [P, 512], BF16)
        o_t = sb("o", [P, 256], F32)
        y_t = sb("y", [P, 256], F32)
        st_t = sb("st", [P, 16], F32)
        el_t = sb("el", [P, 2, 8], F32)
        mrg_t = sb("mrg", [P, NT, D], BF16)
        R_t = sb("R", [P, 4, 512], F32)
        Rb_t = sb("Rb", [P, 4, 512], BF16)
        Sx_t = sb("Sx", [P, 8, P], F32)
        Sb_t = sb("Sb", [P, 8, P], BF16)
        stmp_t = sb("stmp", [P, 2 * P], F32)

        psA = [ps(f"psA{i}", [P, T], F32) for i in range(4)]
        psM = [ps(f"psM{i}", [P, T], F32) for i in range(2)]
        psST_t = ps("psST", [P, T], F32)
        psTb_t = ps("psTb", [P, 1024], BF16)

        sems = {}

        def sem(name):
            if name not in sems:
                sems[name] = es.enter_context(nc.semaphore(name))
            return sems[name]

        for e in Sched.ENGS:
            sem(e)

        xT = [Buf(xT_t, f"xT{c}") for c in range(DC)]
        hT = [Buf(hT_t, f"hT{c}") for c in range(DC)]
        mT = [Buf(mT_t, f"mT{c}") for c in range(DC)]
        a = [Buf(a_t, f"a{f}") for f in range(FH)]
        xin = Buf(xin_t, "xin")
        wsl = [Buf(t, f"ws{i}") for i, t in enumerate(wsl_t)]
        cs = Buf(cs_t, "cs")
        cstb = Buf(cst_t, "cst")
        colsb = Buf(cols_t, "cols")
        lbc = Buf(lbc_t, "lbc")
        ones = Buf(onesr_t, "ones")
        idb = Buf(idb_t, "idb")
        rs = Buf(rs_t, "rs")
        tmp = [Buf(t, "tmp") for t in tmp_t]
        sq = [Buf(t, "sqr") for t in sqr_t]
        sg = tmp[4:6]
        qT = Buf(qT_t, "qT")
        kT = Buf(kT_t, "kT")
        qd = Buf(qd_t, "qd")
        vb = [Buf(v_t, f"v{i}") for i in range(NT)]
        gb = [Buf(gt_t, f"g{i}") for i in range(NT)]
        kd = [Buf(kd_t, f"kd{i}") for i in range(NT)]
        pt = Buf(pt_t, "pt")
        ob = Buf(o_t, "o")
        yb = Buf(y_t, "y")
        stb = Buf(st_t, "st")
        el = Buf(el_t, "el")
        mrg = [Buf(mrg_t, f"mrg{i}") for i in range(NT)]
        Rf = [Buf(R_t, f"R{h}") for h in range(4)]
        Rb = [Buf(Rb_t, f"Rb{h}") for h in range(4)]
        Sf = [Buf(Sx_t, f"S{h}") for h in range(8)]
        Sb = [Buf(Sb_t, f"Sb{h}") for h in range(8)]
        stmp = Buf(stmp_t, "stmp")
        PA = [Buf(t, "psA") for t in psA]
        PM = [Buf(t, "psM") for t in psM]
        PTb = Buf(psTb_t, "psTb")
        PST = Buf(psST_t, "psST")
        xinA = Buf(mT_t, "xinA")
        xinB = Buf(mT_t, "xinB")

        rot = {"A": 0, "M": 0, "w": 0, "sq": 0, "sg": 0, "x": 0}

        def nextA():
            rot["A"] = (rot["A"] + 1) % 4
            return PA[rot["A"]]

        def nextM():
            rot["M"] = (rot["M"] + 1) % 2
            return PM[rot["M"]]

        MASKT = lambda h: cst_t[:, h * 128:(h + 1) * 128]
        QDEC = lambda h: cst_t[:, 512 + h * 128: 512 + (h + 1) * 128]
        BDM = cst_t[:, 1024:1152]
        RESET = cst_t[:, 1152:1664]
        IDF = cst_t[:, 1664:1792]
        KDC = lambda h: cst_t[:, 1792 + h:1793 + h]
        GCOL = lambda k, c: cols_t[:, k * 16 + c:k * 16 + c + 1]
        LB = lambda h: lbc_t[:, h:h + 1]
        OML = lambda h: lbc_t[:, 8 + h:9 + h]
        NOML = lambda h: lbc_t[:, 16 + h:17 + h]

        def load_w(src, ncols, chunk):
            slot = wsl[rot["w"]]
            rot["w"] = (rot["w"] + 1) % 4
            o_ap = slot.t[:, 0:ncols].rearrange("p (a b) -> p a b", b=chunk)
            i_ap = src.rearrange("p (a b) -> p a b", b=chunk)
            S.dma("pool", slot.name, lambda e, o_ap=o_ap, i_ap=i_ap: e.dma_start(out=o_ap, in_=i_ap), writes=[slot])
            return slot

        def mm_group(out_ap, pairs, reads, writes, first=True, last=True, pair_reads=None):
            if pair_reads is not None:
                n = len(pairs)
                for i, (l, r) in enumerate(pairs):
                    S.op("pe", lambda e, l=l, r=r, i=i: e.matmul(out_ap, lhsT=l, rhs=r, start=(first and i == 0), stop=(last and i == n - 1)),
                         reads=pair_reads[i], writes=writes)
                return

            def fn(e, out_ap=out_ap, pairs=pairs, first=first, last=last):
                n = len(pairs)
                ins = None
                for i, (l, r) in enumerate(pairs):
                    ins = e.matmul(out_ap, lhsT=l, rhs=r, start=(first and i == 0), stop=(last and i == n - 1))
                return ins
            S.op("pe", fn, reads=reads, writes=writes)

        def act(out_ap, in_ap, func, reads, writes, **kw):
            S.op("act", lambda e: e.activation(out=out_ap, in_=in_ap, func=func, **kw), reads=reads, writes=writes)

        def tt(out_ap, in0, in1, op, reads, writes):
            S.op("dve", lambda e: e.tensor_tensor(out=out_ap, in0=in0, in1=in1, op=op), reads=reads, writes=writes)

        def ts(out_ap, in0, s1, s2, op0, op1, reads, writes):
            S.op("dve", lambda e: e.tensor_scalar(out=out_ap, in0=in0, scalar1=s1, scalar2=s2, op0=op0, op1=op1),
                 reads=reads, writes=writes)

        def stt(out_ap, in0, scalar, in1, op0, op1, reads, writes):
            S.op("dve", lambda e: e.scalar_tensor_tensor(out=out_ap, in0=in0, scalar=scalar, in1=in1, op0=op0, op1=op1),
                 reads=reads, writes=writes)

        S.dma("sp", "cst", lambda e: e.dma_start(out=cst_t[:], in_=cst[:, :]), writes=[cstb])
        S.dma("sp", "cols", lambda e: e.dma_start(out=cols_t[:], in_=cols[:, :]), writes=[colsb])
        S.op("dve", lambda e: e.memset(tmp_t[5][:, 0:P], 1.0), writes=[tmp[5]])
        S.op("dve", lambda e: e.tensor_copy(out=onesr_t[:], in_=tmp_t[5][:, 0:P]), reads=[tmp[5]], writes=[ones])
        S.op("dve", lambda e: e.tensor_copy(out=idb_t[:], in_=IDF), reads=[cstb], writes=[idb])
        act(lbc_t[:, 0:8], cols_t[:, 80:88], AF.Sigmoid, [colsb], [lbc])
        ts(lbc_t[:, 8:16], lbc_t[:, 0:8], -1.0, 1.0, ALU.mult, ALU.add, [lbc], [lbc])
        ts(lbc_t[:, 16:24], lbc_t[:, 0:8], 1.0, -1.0, ALU.mult, ALU.add, [lbc], [lbc])
        for h in range(4):
            S.op("dve", lambda e, h=h: e.memset(R_t[:, h, :], 0.0), writes=[Rf[h]])
            S.op("dve", lambda e, h=h: e.memset(Rb_t[:, h, :], 0.0), writes=[Rb[h]])
        for h in range(8):
            S.op("dve", lambda e, h=h: e.memset(Sx_t[:, h, :], 0.0), writes=[Sf[h]])
            S.op("dve", lambda e, h=h: e.memset(Sb_t[:, h, :], 0.0), writes=[Sb[h]])

        ones_r = onesr_t[:]
        xA_ap = mT_t[:, 0:8, :].rearrange("p a b -> p (a b)").bitcast(F32)
        xB_ap = mT_t[:, 8:16, :].rearrange("p a b -> p (a b)").bitcast(F32)
        XB = [(xin_t[:], [xin]), (xA_ap, [xinA] + mT[0:8]), (xB_ap, [xinB] + mT[8:16])]

        def load_x(blk):
            for i in range(NT):
                r0 = (blk * NT + i) * P
                xap, xbufs = XB[rot["x"] % 3]
                rot["x"] += 1
                tl = slice(i * P, (i + 1) * P)
                S.dma("sp", "xld" + str(rot["x"] % 3), lambda e, r0=r0, xap=xap: e.dma_start(out=xap, in_=x_all[r0:r0 + P, :]), writes=xbufs)
                for g in range(4):
                    pm = nextM()

                    def fn(e, pm=pm, g=g, xap=xap):
                        ins = None
                        for k in range(4):
                            c = 4 * g + k
                            ins = e.transpose(out=pm.t[:, k * P:(k + 1) * P], in_=xap[:, c * P:(c + 1) * P], identity=IDF)
                        return ins
                    S.op("pe", fn, reads=xbufs + [cstb], writes=[pm])
                    o_ap = xT_t[:, 4 * g:4 * g + 4, tl]
                    i_ap = pm.t[:].rearrange("p (k n) -> p k n", n=P)
                    if g % 2 == 0:
                        S.op("act", lambda e, o_ap=o_ap, i_ap=i_ap: e.copy(out=o_ap, in_=i_ap), reads=[pm], writes=xT[4 * g:4 * g + 4])
                    else:
                        S.op("dve", lambda e, o_ap=o_ap, i_ap=i_ap: e.tensor_copy(out=o_ap, in_=i_ap), reads=[pm], writes=xT[4 * g:4 * g + 4])
            for c in range(DC):
                norm_sq(c)
                if c >= 1:
                    norm_mm(c - 1)
            norm_mm(DC - 1)
            norm_finish(0)

        def norm_sq(c):
            s_ = sq[c % 2]
            act(s_.t[:], xT_t[:, c, :], AF.Square, [xT[c]], [s_])

        def norm_mm(c):
            s_ = sq[c % 2]
            mm_group(PST.t[:], [(ones_r, s_.t[:])], [ones, s_], [PST], first=(c == 0), last=(c == DC - 1))

        def norm_finish(k, final=False):
            act(rs_t[:], PST.t[:], AF.Sqrt, [PST], [rs], scale=1.0 / D, bias=EPS)
            S.op("dve", lambda e: e.reciprocal(out=rs_t[:], in_=rs_t[:]), reads=[rs], writes=[rs])
            for c in range(DC):
                if final:
                    stt(xT_t[:, c, :], xT_t[:, c, :], GCOL(k, c), rs_t[:], ALU.mult, ALU.mult, [xT[c], colsb, rs], [xT[c]])
                else:
                    stt(hT_t[:, c, :], xT_t[:, c, :], GCOL(k, c), rs_t[:], ALU.mult, ALU.mult, [xT[c], colsb, rs], [hT[c]])

        def ffn(which):
            for half in range(2):
                for fl in range(FH):
                    f = half * FH + fl
                    slot = load_w(wgu[which][f], 4096, 2048)
                    pg = nextA()
                    pu = nextA()
                    if f == 0:
                        pr = [[slot, hT[c]] for c in range(DC)]
                        mm_group(pg.t[:], [(slot.t[:, c * P:(c + 1) * P], hT_t[:, c, :]) for c in range(DC)], [slot] + hT, [pg], pair_reads=pr)
                        mm_group(pu.t[:], [(slot.t[:, 2048 + c * P:2048 + (c + 1) * P], hT_t[:, c, :]) for c in range(DC)], [slot] + hT, [pu])
                    else:
                        def fn(e, slot=slot, pg=pg, pu=pu):
                            ins = None
                            for (pp, off) in ((pg, 0), (pu, 2048)):
                                for c in range(DC):
                                    ins = e.matmul(pp.t[:], lhsT=slot.t[:, off + c * P:off + (c + 1) * P], rhs=hT_t[:, c, :],
                                                   start=(c == 0), stop=(c == DC - 1))
                            return ins
                        S.op("pe", fn, reads=[slot] + hT, writes=[pg, pu])
                    s = sg[fl % 2]
                    act(s.t[:], pg.t[:], AF.Silu, [pg], [s])
                    tt(a_t[:, fl, :], s.t[:], pu.t[:], ALU.mult, [s, pu], [a[fl]])
                for c in range(DC):
                    slot = load_w(wdn[which][half * DC + c], FH * P, 1408)
                    py = nextA()
                    mm_group(py.t[:], [(slot.t[:, fl * P:(fl + 1) * P], a_t[:, fl, :]) for fl in range(FH)], [slot] + a, [py])
                    stt(xT_t[:, c, :], py.t[:], 0.5, xT_t[:, c, :], ALU.mult, ALU.add, [py, xT[c]], [xT[c]])
                    if half == 1:
                        norm_sq(c)
                        if c >= 1:
                            norm_mm(c - 1)
                if half == 1:
                    norm_mm(DC - 1)

        def tm_proj(u, half, dst_t, dst, func):
            slot = load_w(wtm[2 * u + half], 4096, 2048)
            for i in range(NT):
                pa = nextA()
                o_ap = pa.t[:, 0:256]
                mm_group(o_ap, [(hT_t[:, c, i * P:(i + 1) * P], slot.t[:, c * 256:(c + 1) * 256]) for c in range(DC)],
                         [slot] + hT, [pa])
                if func is None:
                    S.op("act", lambda e, o_ap=o_ap, i=i: e.copy(out=dst_t[:, i, :], in_=o_ap), reads=[pa], writes=[dst[i]])
                else:
                    act(dst_t[:, i, :], o_ap, func, [pa], [dst[i]])

        def fm_proj(slot, k, split=False):
            pa = nextA()
            pr = [[slot, hT[c]] for c in range(DC)] if split else None
            mm_group(pa.t[:], [(slot.t[:, (k * DC + c) * P:(k * DC + c + 1) * P], hT_t[:, c, :]) for c in range(DC)], [slot] + hT, [pa],
                     pair_reads=pr)
            return pa

        def rope(slot, dst_t, dst, split=False):
            p1 = fm_proj(slot, 0, split)
            p2 = fm_proj(slot, 1)
            COS = cs_t[:, 0:T]
            SIN = cs_t[:, T:2 * T]
            tt(tmp_t[0][:], p1.t[:], COS, ALU.mult, [p1, cs], [tmp[0]])
            tt(tmp_t[3][:], p1.t[:], SIN, ALU.mult, [p1, cs], [tmp[3]])
            tt(tmp_t[1][:], p2.t[:], SIN, ALU.mult, [p2, cs], [tmp[1]])
            tt(tmp_t[2][:], p2.t[:], COS, ALU.mult, [p2, cs], [tmp[2]])
            tt(dst_t[:, 0, :], tmp_t[0][:], tmp_t[1][:], ALU.subtract, [tmp[0], tmp[1]], [dst])
            tt(dst_t[:, 1, :], tmp_t[2][:], tmp_t[3][:], ALU.add, [tmp[2], tmp[3]], [dst])

        def ret_unit(h, full, need_bf=True):
            if full:
                sq_ = load_w(wfm[2 * h], 4096, 2048)
                rope(sq_, qT_t, qT, split=(h == 0))
                for cc in range(2):
                    for i in range(NT):
                        tt(qd_t[:, cc, i * P:(i + 1) * P], qT_t[:, cc, i * P:(i + 1) * P], QDEC(h), ALU.mult, [qT, cstb], [qd])
            sk = load_w(wfm[2 * h + 1], 4096, 2048)
            rope(sk, kT_t, kT, split=(h == 0 and not full))
            tm_proj(h, 0, v_t, vb, None)
            if full:
                tm_proj(h, 1, gt_t, gb, AF.Silu)
            def fn(e):
                ins = None
                for i in range(NT):
                    for cc in range(2):
                        ins = e.transpose(out=psTb_t[:, (2 * i + cc) * P:(2 * i + cc + 1) * P], in_=kT_t[:, cc, i * P:(i + 1) * P],
                                          identity=idb_t[:])
                return ins
            S.op("pe", fn, reads=[kT, idb], writes=[PTb])
            ts(kd_t[:].rearrange("p a b -> p (a b)"), psTb_t[:], KDC(h), None, ALU.mult, ALU.bypass, [PTb, cstb], kd)
            if full:
                pm = nextM()
                for i in range(NT):
                    tsl = slice(i * P, (i + 1) * P)
                    mm_group(pm.t[:, tsl], [(kT_t[:, cc, tsl], qT_t[:, cc, tsl]) for cc in range(2)], [kT, qT], [pm])
                for i in range(NT):
                    tsl = slice(i * P, (i + 1) * P)
                    tt(pt_t[:, tsl], pm.t[:, tsl], MASKT(h), ALU.mult, [pm, cstb], [pt])
            for i in range(NT):
                tsl = slice(i * P, (i + 1) * P)
                if full:
                    po = nextM()
                    mm_group(po.t[:, 0:256], [(pt_t[:, tsl], v_t[:, i, :])], [pt, vb[i]], [po], first=True, last=False)
                pk = nextA()
                for cc in range(2):
                    mm_group(pk.t[:, cc * 256:(cc + 1) * 256], [(kd_t[:, i, cc * P:(cc + 1) * P], v_t[:, i, :])], [kd[i], vb[i]], [pk])
                if full:
                    mm_group(po.t[:, 0:256], [(qd_t[:, cc, tsl], Rb_t[:, h, cc * 256:(cc + 1) * 256]) for cc in range(2)],
                             [qd, Rb[h]], [po], first=False, last=True)
                    S.op("act", lambda e, po=po: e.copy(out=o_t[:], in_=po.t[:, 0:256]), reads=[po], writes=[ob])
                    stt(Rb_t[:, h, :], R_t[:, h, :], GAM[h] ** 128, pk.t[:], ALU.mult, ALU.add, [Rf[h], pk], [Rb[h]])
                stt(R_t[:, h, :], R_t[:, h, :], GAM[h] ** 128, pk.t[:], ALU.mult, ALU.add, [Rf[h], pk], [Rf[h]])
                if need_bf and (not full) and i == NT - 1:
                    S.op("act", lambda e, h=h: e.copy(out=Rb_t[:, h, :], in_=R_t[:, h, :]), reads=[Rf[h]], writes=[Rb[h]])
                if full:
                    S.op("dve", lambda e: e.bn_stats(out=st_t[:, 0:6], in_=o_t[:]), reads=[ob], writes=[stb])
                    S.op("dve", lambda e: e.bn_aggr(out=st_t[:, 8:10], in_=st_t[:, 0:6]), reads=[stb], writes=[stb])
                    act(st_t[:, 10:11], st_t[:, 9:10], AF.Sqrt, [stb], [stb], bias=EPS)
                    S.op("dve", lambda e: e.reciprocal(out=st_t[:, 10:11], in_=st_t[:, 10:11]), reads=[stb], writes=[stb])
                    ts(y_t[:], o_t[:], st_t[:, 8:9], st_t[:, 10:11], ALU.subtract, ALU.mult, [ob, stb], [yb])
                    tt(mrg_t[:, i, h * 256:(h + 1) * 256], y_t[:], gt_t[:, i, :], ALU.mult, [yb, gb[i]], [mrg[i]])

        def hgrn_unit(hp, full, need_bf=True):
            u = 4 + hp
            sf = load_w(wfm[2 * u + 1], 4096, 2048)
            if full:
                sq_ = load_w(wfm[2 * u], 4096, 2048)
            for hh in range(2):
                hd = 2 * hp + hh
                pz = fm_proj(sf, hh)
                act(tmp_t[0][:], pz.t[:], AF.Sigmoid, [pz], [tmp[0]])
                ts(tmp_t[1][:], tmp_t[0][:], OML(hd), LB(hd), ALU.mult, ALU.add, [tmp[0], lbc], [tmp[1]])
                act(tmp_t[1][:], tmp_t[1][:], AF.Ln, [tmp[1]], [tmp[1]])
                S.op("dve", lambda e: e.tensor_tensor_scan(out=tmp_t[2][:], data0=RESET, data1=tmp_t[1][:], initial=0.0,
                                                           op0=ALU.mult, op1=ALU.add), reads=[tmp[1], cstb], writes=[tmp[2]])
                act(tmp_t[3][:], tmp_t[2][:], AF.Exp, [tmp[2]], [tmp[3]], scale=-1.0)
                ts(tmp_t[4][:], tmp_t[0][:], NOML(hd), OML(hd), ALU.mult, ALU.add, [tmp[0], lbc], [tmp[4]])
                tt(kT_t[:, hh, :], tmp_t[4][:], tmp_t[3][:], ALU.mult, [tmp[4], tmp[3]], [kT])
                act(tmp_t[5][:], tmp_t[2][:], AF.Exp, [tmp[2]], [tmp[5]])
                ecv = tmp_t[5][:].rearrange("p (c j) -> p c j", j=64)[:, :, 63:64]
                S.op("dve", lambda e, hh=hh, ecv=ecv: e.tensor_copy(out=el_t[:, hh, :].rearrange("p (c o) -> p c o", o=1), in_=ecv),
                     reads=[tmp[5]], writes=[el])
                if full:
                    pq = fm_proj(sq_, hh)
                    act(tmp_t[0][:], pq.t[:], AF.Silu, [pq], [tmp[0]])
                    tt(qT_t[:, hh, :], tmp_t[0][:], tmp_t[5][:], ALU.mult, [tmp[0], tmp[5]], [qT])
            tm_proj(u, 0, v_t, vb, None)
            if full:
                tm_proj(u, 1, gt_t, gb, AF.Silu)
            def fn(e):
                ins = None
                for i in range(NT):
                    for hh in range(2):
                        ins = e.transpose(out=psTb_t[:, (2 * i + hh) * P:(2 * i + hh + 1) * P], in_=kT_t[:, hh, i * P:(i + 1) * P],
                                          identity=idb_t[:])
                return ins
            S.op("pe", fn, reads=[kT, idb], writes=[PTb])
            S.op("act", lambda e: e.copy(out=kd_t[:].rearrange("p a b -> p (a b)"), in_=psTb_t[:]), reads=[PTb], writes=kd)
            ptq = qd_t[:].rearrange("p a b -> p (a b)")
            if full:
                pms = [nextA(), nextA()]
                for i in range(NT):
                    tsl = slice(i * P, (i + 1) * P)
                    for hh in range(2):
                        col = ((i % 2) * 2 + hh) * P
                        mm_group(pms[i // 2].t[:, col:col + P], [(kT_t[:, hh, tsl], qT_t[:, hh, tsl])], [kT, qT], [pms[i // 2]])
                for i in range(NT):
                    for hh in range(2):
                        col = ((i % 2) * 2 + hh) * P
                        tt(ptq[:, (i * 2 + hh) * P:(i * 2 + hh + 1) * P], pms[i // 2].t[:, col:col + P], BDM, ALU.mult, [pms[i // 2], cstb], [qd])
            for i in range(NT):
                tsl = slice(i * P, (i + 1) * P)
                po = [None, None]
                if full:
                    po = [nextA(), nextA()]
                if full:
                    for c2 in range(2):
                        for hh in range(2):
                            hs = slice(hh * P, (hh + 1) * P)
                            mm_group(po[hh].t[c2 * 64:(c2 + 1) * 64, 0:P],
                                     [(ptq[:, (i * 2 + hh) * P + c2 * 64:(i * 2 + hh) * P + (c2 + 1) * 64], v_t[:, i, hs])],
                                     [qd, vb[i]], [po[hh]], first=True, last=False)
                for c2 in range(2):
                    pk = nextA()
                    pko = 0
                    for hh in range(2):
                        vs = slice(hh * P, (hh + 1) * P)
                        mm_group(pk.t[:, vs], [(kd_t[c2 * 64:(c2 + 1) * 64, i, vs], v_t[c2 * 64:(c2 + 1) * 64, i, vs])], [kd[i], vb[i]], [pk])
                    if full:
                        for hh in range(2):
                            hd = 2 * hp + hh
                            mm_group(po[hh].t[c2 * 64:(c2 + 1) * 64, 0:P],
                                     [(qT_t[:, hh, i * P + c2 * 64:i * P + (c2 + 1) * 64], Sb_t[:, hd, :])],
                                     [qT, Sb[hd]], [po[hh]], first=False, last=True)
                    h0 = 2 * hp
                    tt(stmp_t[:], pk.t[:, pko:pko + 256], Sx_t[:, h0:h0 + 2, :].rearrange("p a b -> p (a b)"), ALU.add, [pk, Sf[h0], Sf[h0 + 1]], [stmp])
                    for hh in range(2):
                        hd = 2 * hp + hh
                        hs = slice(hh * P, (hh + 1) * P)
                        ecol = el_t[:, hh, 2 * i + c2:2 * i + c2 + 1]
                        if full:
                            ts(Sb_t[:, hd, :], stmp_t[:, hs], ecol, None, ALU.mult, ALU.bypass, [stmp, el], [Sb[hd]])
                        ts(Sx_t[:, hd, :], stmp_t[:, hs], ecol, None, ALU.mult, ALU.bypass, [stmp, el], [Sf[hd]])
                        if need_bf and (not full) and i == NT - 1 and c2 == 1:
                            S.op("act", lambda e, hd=hd: e.copy(out=Sb_t[:, hd, :], in_=Sx_t[:, hd, :]), reads=[Sf[hd]], writes=[Sb[hd]])
                if full:
                    for hh in range(2):
                        hs = slice(hh * P, (hh + 1) * P)
                        S.op("act", lambda e, hh=hh, hs=hs, p_=po[hh]: e.copy(out=o_t[:, hs], in_=p_.t[:, 0:P]), reads=[po[hh]], writes=[ob])
                        S.op("act", lambda e, hh=hh, hs=hs: e.activation(out=y_t[:, hs], in_=o_t[:, hs], func=AF.Square,
                                                                        accum_out=st_t[:, hh:hh + 1]), reads=[ob], writes=[yb, stb])
                    act(st_t[:, 2:4], st_t[:, 0:2], AF.Sqrt, [stb], [stb], scale=1.0 / P, bias=EPS)
                    S.op("dve", lambda e: e.reciprocal(out=st_t[:, 2:4], in_=st_t[:, 2:4]), reads=[stb], writes=[stb])
                    for hh in range(2):
                        hs = slice(hh * P, (hh + 1) * P)
                        stt(mrg_t[:, i, u * 256 + hh * P:u * 256 + (hh + 1) * P], o_t[:, hs], st_t[:, 2 + hh:3 + hh], gt_t[:, i, hs],
                            ALU.mult, ALU.mult, [ob, stb, gb[i]], [mrg[i]])

        def mixer(blk, full):
            norm_finish(1)
            S.dma("sp", "csl", lambda e, blk=blk: e.dma_start(out=cs_t[:], in_=cs_all[blk]), writes=[cs])
            if DBG >= 4:
                for h in range(4):
                    ret_unit(h, full, full or blk == N_PRE - 1)
            if DBG >= 5:
                for hp in range(4):
                    hgrn_unit(hp, full, full or blk == N_PRE - 1)
            if not full or DBG < 6:
                return
            for i in range(NT):
                for half in range(2):
                    def fn(e, i=i, half=half):
                        ins = None
                        for k in range(8):
                            m = 8 * half + k
                            ins = e.transpose(out=psTb_t[:, k * P:(k + 1) * P], in_=mrg_t[:, i, m * P:(m + 1) * P], identity=idb_t[:])
                        return ins
                    S.op("pe", fn, reads=[mrg[i], idb], writes=[PTb])
                    for k in range(8):
                        m = 8 * half + k
                        if True:
                            ts(mT_t[:, m, i * P:(i + 1) * P], psTb_t[:, k * P:(k + 1) * P], GCOL(4, m), None, ALU.mult, ALU.bypass,
                               [PTb, colsb], [mT[m]])
                        else:
                            S.op("act", lambda e, m=m, k=k, i=i: e.activation(out=mT_t[:, m, i * P:(i + 1) * P], in_=psTb_t[:, k * P:(k + 1) * P],
                                                                              func=AF.Copy, scale=GCOL(4, m)),
                                 reads=[PTb, colsb], writes=[mT[m]])
            for g in range(8):
                slot = load_w(wout[g], 4096, 2048)
                for k in range(2):
                    c = 2 * g + k
                    pa = nextA()
                    mm_group(pa.t[:], [(slot.t[:, (k * DC + m) * P:(k * DC + m + 1) * P], mT_t[:, m, :]) for m in range(DC)], [slot] + mT, [pa])
                    tt(xT_t[:, c, :], xT_t[:, c, :], pa.t[:], ALU.add, [xT[c], pa], [xT[c]])
                    norm_sq(c)
                    if c >= 1:
                        norm_mm(c - 1)
            norm_mm(DC - 1)

        def store_out(ob_i):
            norm_finish(3, final=True)
            for i in range(NT):
                xi = rot["x"] % 3
                xap, xbufs = XB[xi]
                rot["x"] += 1
                for g in range(4):
                    pm = nextM()

                    def fn(e, pm=pm, g=g, i=i):
                        ins = None
                        for k in range(4):
                            c = 4 * g + k
                            ins = e.transpose(out=pm.t[:, k * P:(k + 1) * P], in_=xT_t[:, c, i * P:(i + 1) * P], identity=IDF)
                        return ins
                    S.op("pe", fn, reads=xT[4 * g:4 * g + 4] + [cstb], writes=[pm])
                    if g % 2 == 0:
                        S.op("act", lambda e, pm=pm, g=g, xap=xap: e.copy(out=xap[:, g * T:(g + 1) * T], in_=pm.t[:]), reads=[pm], writes=xbufs)
                    else:
                        S.op("dve", lambda e, pm=pm, g=g, xap=xap: e.tensor_copy(out=xap[:, g * T:(g + 1) * T], in_=pm.t[:]), reads=[pm], writes=xbufs)
                r0 = (ob_i * NT + i) * P
                S.dma("sp", "xst" + str(xi), lambda e, r0=r0, xap=xap: e.dma_start(out=out[r0:r0 + P, :], in_=xap), reads=xbufs)

        for blk in range(NB):
            full = blk >= N_PRE
            load_x(blk)
            if DBG >= 2:
                ffn(0)
            if DBG >= 3:
                mixer(blk, full)
            if full:
                if DBG >= 7:
                    norm_finish(2)
                    ffn(1)
                store_out(blk - N_PRE)
        S.wait("sp", [(k, v) for k, v in S.dcount.items() if k.startswith("xst")])

        with nc.Block() as block:
            def replay(name):
                def body(e):
                    for waits, fn, inc in S.streams[name]:
                        for (s, v) in waits:
                            e.wait_ge(sem(s), v)
                        if fn is not None:
                            ins = fn(e)
                            ins.then_inc(sem(inc[0]), inc[1])
                return body
            block.tensor(replay("pe"))
            block.scalar(replay("act"))
            block.vector(replay("dve"))
            block.gpsimd(replay("pool"))
            block.sync(replay("sp"))
    return nc


def _fm(W, starts):
    parts = [W[:, s:s + P].reshape(DC, P, P).transpose(1, 0, 2) for s in starts]
    return np.ascontiguousarray(np.stack(parts, axis=1).reshape(P, -1))


def _tm(W, s0):
    return np.ascontiguousarray(W[:, s0:s0 + 256].reshape(DC, P, 256).transpose(1, 0, 2).reshape(P, -1))


def _prep_weights(inp):
    f32 = np.float32
    w = {}
    for idx, pre in enumerate(["ffn1", "ffn2"]):
        Wg = np.asarray(inp[pre + "_w_gate"][0], f32)
        Wu = np.asarray(inp[pre + "_w_up"][0], f32)
        Wd = np.asarray(inp[pre + "_w_down"][0], f32)
        g = Wg.reshape(DC, P, FC, P).transpose(2, 1, 0, 3)
        u = Wu.reshape(DC, P, FC, P).transpose(2, 1, 0, 3)
        w[f"wgu{idx + 1}"] = np.ascontiguousarray(np.stack([g, u], axis=2).reshape(FC, P, 4096))
        d = Wd.reshape(2, FH, P, DC, P).transpose(0, 3, 2, 1, 4)
        w[f"wd{idx + 1}"] = np.ascontiguousarray(d.reshape(2 * DC, P, FH * P))
    Win = np.asarray(inp["w_in"][0], f32)
    fm, tm = [], []
    for u_ in range(8):
        if u_ < 4:
            qb, kb, vbs, gbs = 0 + u_ * 256, 1024 + u_ * 256, 2048 + u_ * 256, 3072 + u_ * 256
        else:
            hp = u_ - 4
            qb, kb, vbs, gbs = 4096 + hp * 256, 5120 + hp * 256, 6144 + hp * 256, 7168 + hp * 256
        fm.append(_fm(Win, [qb, qb + P]))
        fm.append(_fm(Win, [kb, kb + P]))
        tm.append(_tm(Win, vbs))
        tm.append(_tm(Win, gbs))
    w["wfm"] = np.stack(fm)
    w["wtm"] = np.stack(tm)
    Wo = np.asarray(inp["w_out"][0], f32)
    w["wout"] = np.stack([_fm(Wo, [2 * g * P, (2 * g + 1) * P]) for g in range(8)])
    return w


def _consts():
    c = np.zeros((P, 1800), np.float32)
    j = np.arange(P)[:, None].astype(np.float64)
    i = np.arange(P)[None, :].astype(np.float64)
    for h in range(4):
        g = GAM[h]
        m = np.where(i >= j, g ** np.maximum(i - j, 0.0), 0.0) / 16.0
        c[:, h * P:(h + 1) * P] = m
        row = (g ** (np.arange(P, dtype=np.float64) + 1.0)) / 16.0
        c[:, 512 + h * 128:512 + (h + 1) * 128] = row[None, :]
        c[:, 1792 + h] = g ** (127.0 - np.arange(P, dtype=np.float64))
    same = (np.arange(P)[:, None] // 64) == (np.arange(P)[None, :] // 64)
    c[:, 1024:1152] = np.where(same & (i >= j), 1.0, 0.0)
    c[:, 1152:1664] = np.where(np.arange(T) % 64 == 0, 0.0, 1.0)[None, :]
    c[:, 1664:1792] = np.eye(P)
    return c


def _rope_tables(pos0, NB=NB):
    inv = np.power(np.float32(10000.0), -(np.arange(0, 256, 2, dtype=np.float32) / np.float32(256.0))).astype(np.float32)
    tabs = np.zeros((NB, P, 2 * T), np.float32)
    for b in range(NB):
        pos = (pos0 + b * T + np.arange(T)).astype(np.float32)
        ang = (pos[None, :] * inv[:, None]).astype(np.float32).astype(np.float64)
        tabs[b, :, 0:T] = np.cos(ang)
        tabs[b, :, T:] = np.sin(ang)
    return tabs


_NC_CACHE = {}


def kernel(**inp):
    x = np.asarray(inp["x"], np.float32)
    w = _prep_weights(inp)
    cst = _consts()
    cols = np.zeros((P, 128), np.float32)
    for k, name in enumerate(["ffn1_norm", "mix_norm", "ffn2_norm", "final_norm"]):
        cols[:, k * 16:(k + 1) * 16] = np.asarray(inp[name], np.float32).reshape(DC, P).T
    gm = np.concatenate([np.asarray(inp["ret_norm_g"], np.float32).reshape(-1), np.asarray(inp["hgrn_norm_g"], np.float32).reshape(-1)])
    cols[:, 64:80] = gm.reshape(DC, P).T
    cols[:, 80:88] = np.asarray(inp["hgrn_lb_logits"], np.float32).reshape(8, P).T
    in_maps = []
    for core in range(8):
        b, j = core // 4, core % 4
        xa = np.zeros((NB * T, D), np.float32)
        lo = j * SEG - N_PRE * T
        src_lo = max(lo, 0)
        xa[src_lo - lo:, :] = x[b, src_lo:(j + 1) * SEG, :]
        m = dict(w)
        m["x_all"] = xa
        m["cs_all"] = _rope_tables(lo)
        m["cst"] = cst
        m["cols"] = cols
        in_maps.append(m)
    if "nc" not in _NC_CACHE:
        _NC_CACHE["nc"] = build_nc()
    res = run_bass_kernel_spmd(_NC_CACHE["nc"], in_maps, core_ids=list(range(8)))
    outp = np.zeros((2, SEQ, D), np.float32)
    for core in range(8):
        b, j = core // 4, core % 4
        outp[b, j * SEG:(j + 1) * SEG, :] = res.results[core]["out"]
    return outp
```

```python
import numpy as np
import concourse.bass as bass
import concourse.mybir as mybir
from concourse.bass_utils import run_bass_kernel_spmd

F32 = mybir.dt.float32
F32R = mybir.dt.float32r
BF16 = mybir.dt.bfloat16
AF = mybir.ActivationFunctionType
ALU = mybir.AluOpType

P = 128
D = 2048
DC = 16
FF = 5632
FC = 44
FH = 22
T = 512
NT = 4
SEQ = 8192
SEG = 2048
N_PRE = 12
N_OWN = 4
NB = N_PRE + N_OWN
EPS = 1e-6
DBG = 9
GAM = [1.0 - 2.0 ** (-5 - h) for h in range(4)]


class Buf:
    def __init__(self, t, name):
        self.t = t
        self.name = name
        self.w = set()
        self.r = set()


class Sched:
    ENGS = ["pe", "act", "dve", "pool", "sp"]

    def __init__(self):
        self.streams = {e: [] for e in self.ENGS}
        self.count = {e: 0 for e in self.ENGS}
        self.waited = {e: {} for e in self.ENGS}
        self.dcount = {}

    def _waits(self, eng, deps):
        need = {}
        for (s, v) in deps:
            if s == eng and eng == "pe":
                continue
            if v > need.get(s, 0):
                need[s] = v
        out = []
        wd = self.waited[eng]
        for s, v in need.items():
            if wd.get(s, 0) < v:
                wd[s] = v
                out.append((s, v))
        return out

    def op(self, eng, fn, reads=(), writes=()):
        deps = set()
        for b in reads:
            deps |= b.w
        for b in writes:
            deps |= b.w
            deps |= b.r
        waits = self._waits(eng, deps)
        self.count[eng] += 1
        tok = (eng, self.count[eng])
        self.streams[eng].append((waits, fn, (eng, 1)))
        for b in writes:
            b.w = {tok}
            b.r = set()
        for b in reads:
            if b not in writes:
                b.r.add(tok)

    def dma(self, eng, sem, fn, reads=(), writes=()):
        deps = set()
        for b in reads:
            deps |= b.w
        for b in writes:
            deps |= b.w
            deps |= b.r
        waits = self._waits(eng, deps)
        self.dcount[sem] = self.dcount.get(sem, 0) + 16
        tok = (sem, self.dcount[sem])
        self.streams[eng].append((waits, fn, (sem, 16)))
        for b in writes:
            b.w = {tok}
            b.r = set()
        for b in reads:
            if b not in writes:
                b.r.add(tok)
        return tok

    def wait(self, eng, toks):
        waits = self._waits(eng, set(toks))
        self.streams[eng].append((waits, None, None))


def build_nc(N_PRE=N_PRE, N_OWN=N_OWN):
    NB = N_PRE + N_OWN
    nc = bass.Bass("TRN2", target_bir_lowering=False)
    S = Sched()

    def dram(name, shape, kind="ExternalInput", dt=F32):
        return nc.dram_tensor(name, list(shape), dt, kind=kind).ap()

    x_all = dram("x_all", [NB * T, D])
    cs_all = dram("cs_all", [NB, P, 2 * T])
    wgu = [dram("wgu1", [FC, P, 4096]), dram("wgu2", [FC, P, 4096])]
    wdn = [dram("wd1", [2 * DC, P, FH * P]), dram("wd2", [2 * DC, P, FH * P])]
    wfm = dram("wfm", [16, P, 4096])
    wtm = dram("wtm", [16, P, 4096])
    wout = dram("wout", [8, P, 4096])
    cst = dram("cst", [P, 1800])
    cols = dram("cols", [P, 128])
    out = dram("out", [N_OWN * T, D], kind="ExternalOutput")

    import contextlib
    es = contextlib.ExitStack()

    def sb(name, shape, dt):
        return es.enter_context(nc.sbuf_tensor(name, list(shape), dt))

    def ps(name, shape, dt):
        return es.enter_context(nc.psum_tensor(name, list(shape), dt))

    with es:
        xT_t = sb("xT", [P, DC, T], F32)
        hT_t = sb("hT", [P, DC, T], BF16)
        mT_t = sb("mT", [P, DC, T], BF16)
        a_t = sb("a", [P, FH, T], BF16)
        xin_t = sb("xin", [P, D], F32)
        wsl_t = [sb(f"ws{i}", [P, 4096], BF16) for i in range(4)]
        cs_t = sb("cs", [P, 2 * T], F32)
        cst_t = sb("cstt", [P, 1800], F32)
        cols_t = sb("colst", [P, 128], F32)
        lbc_t = sb("lbc", [P, 32], F32)
        idb_t = sb("idb", [P, P], BF16)
        rs_t = sb("rs", [P, T], F32)
        sqr_t = [sb(f"sqr{i}", [P, T], F32R) for i in range(2)]
        onesr_t = sb("onesr", [P, P], F32R)
        tmp_t = [sb(f"tmp{i}", [P, T], F32) for i in range(6)]
        qT_t = sb("qT", [P, 2, T], BF16)
        kT_t = sb("kT", [P, 2, T], BF16)
        qd_t = sb("qd", [P, 2, T], BF16)
        v_t = sb("v", [P, NT, 256], BF16)
        gt_t = sb("gt", [P, NT, 256], BF16)
        kd_t = sb("kd", [P, NT, 256], BF16)
        pt_t = sb("pt", [P, 512], BF16)
        o_t = sb("o", [P, 256], F32)
        y_t = sb("y", [P, 256], F32)
        st_t = sb("st", [P, 16], F32)
        el_t = sb("el", [P, 2, 8], F32)
        mrg_t = sb("mrg", [P, NT, D], BF16)
        R_t = sb("R", [P, 4, 512], F32)
        Rb_t = sb("Rb", [P, 4, 512], BF16)
        Sx_t = sb("Sx", [P, 8, P], F32)
        Sb_t = sb("Sb", [P, 8, P], BF16)
        stmp_t = sb("stmp", [P, 2 * P], F32)

        psA = [ps(f"psA{i}", [P, T], F32) for i in range(4)]
        psM = [ps(f"psM{i}", [P, T], F32) for i in range(2)]
        psST_t = ps("psST", [P, T], F32)
        psTb_t = ps("psTb", [P, 1024], BF16)

        sems = {}

        def sem(name):
            if name not in sems:
                sems[name] = es.enter_context(nc.semaphore(name))
            return sems[name]

        for e in Sched.ENGS:
            sem(e)

        xT = [Buf(xT_t, f"xT{c}") for c in range(DC)]
        hT = [Buf(hT_t, f"hT{c}") for c in range(DC)]
        mT = [Buf(mT_t, f"mT{c}") for c in range(DC)]
        a = [Buf(a_t, f"a{f}") for f in range(FH)]
        xin = Buf(xin_t, "xin")
        wsl = [Buf(t, f"ws{i}") for i, t in enumerate(wsl_t)]
        cs = Buf(cs_t, "cs")
        cstb = Buf(cst_t, "cst")
        colsb = Buf(cols_t, "cols")
        lbc = Buf(lbc_t, "lbc")
        ones = Buf(onesr_t, "ones")
        idb = Buf(idb_t, "idb")
        rs = Buf(rs_t, "rs")
        tmp = [Buf(t, "tmp") for t in tmp_t]
        sq = [Buf(t, "sqr") for t in sqr_t]
        sg = tmp[4:6]
        qT = Buf(qT_t, "qT")
        kT = Buf(kT_t, "kT")
        qd = Buf(qd_t, "qd")
        vb = [Buf(v_t, f"v{i}") for i in range(NT)]
        gb = [Buf(gt_t, f"g{i}") for i in range(NT)]
        kd = [Buf(kd_t, f"kd{i}") for i in range(NT)]
        pt = Buf(pt_t, "pt")
        ob = Buf(o_t, "o")
        yb = Buf(y_t, "y")
        stb = Buf(st_t, "st")
        el = Buf(el_t, "el")
        mrg = [Buf(mrg_t, f"mrg{i}") for i in range(NT)]
        Rf = [Buf(R_t, f"R{h}") for h in range(4)]
        Rb = [Buf(Rb_t, f"Rb{h}") for h in range(4)]
        Sf = [Buf(Sx_t, f"S{h}") for h in range(8)]
        Sb = [Buf(Sb_t, f"Sb{h}") for h in range(8)]
        stmp = Buf(stmp_t, "stmp")
        PA = [Buf(t, "psA") for t in psA]
        PM = [Buf(t, "psM") for t in psM]
        PTb = Buf(psTb_t, "psTb")
        PST = Buf(psST_t, "psST")
        xinA = Buf(mT_t, "xinA")
        xinB = Buf(mT_t, "xinB")

        rot = {"A": 0, "M": 0, "w": 0, "sq": 0, "sg": 0, "x": 0}

        def nextA():
            rot["A"] = (rot["A"] + 1) % 4
            return PA[rot["A"]]

        def nextM():
            rot["M"] = (rot["M"] + 1) % 2
            return PM[rot["M"]]

        MASKT = lambda h: cst_t[:, h * 128:(h + 1) * 128]
        QDEC = lambda h: cst_t[:, 512 + h * 128: 512 + (h + 1) * 128]
        BDM = cst_t[:, 1024:1152]
        RESET = cst_t[:, 1152:1664]
        IDF = cst_t[:, 1664:1792]
        KDC = lambda h: cst_t[:, 1792 + h:1793 + h]
        GCOL = lambda k, c: cols_t[:, k * 16 + c:k * 16 + c + 1]
        LB = lambda h: lbc_t[:, h:h + 1]
        OML = lambda h: lbc_t[:, 8 + h:9 + h]
        NOML = lambda h: lbc_t[:, 16 + h:17 + h]

        def load_w(src, ncols, chunk):
            slot = wsl[rot["w"]]
            rot["w"] = (rot["w"] + 1) % 4
            o_ap = slot.t[:, 0:ncols].rearrange("p (a b) -> p a b", b=chunk)
            i_ap = src.rearrange("p (a b) -> p a b", b=chunk)
            S.dma("pool", slot.name, lambda e, o_ap=o_ap, i_ap=i_ap: e.dma_start(out=o_ap, in_=i_ap), writes=[slot])
            return slot

        def mm_group(out_ap, pairs, reads, writes, first=True, last=True, pair_reads=None):
            if pair_reads is not None:
                n = len(pairs)
                for i, (l, r) in enumerate(pairs):
                    S.op("pe", lambda e, l=l, r=r, i=i: e.matmul(out_ap, lhsT=l, rhs=r, start=(first and i == 0), stop=(last and i == n - 1)),
                         reads=pair_reads[i], writes=writes)
                return

            def fn(e, out_ap=out_ap, pairs=pairs, first=first, last=last):
                n = len(pairs)
                ins = None
                for i, (l, r) in enumerate(pairs):
                    ins = e.matmul(out_ap, lhsT=l, rhs=r, start=(first and i == 0), stop=(last and i == n - 1))
                return ins
            S.op("pe", fn, reads=reads, writes=writes)

        def act(out_ap, in_ap, func, reads, writes, **kw):
            S.op("act", lambda e: e.activation(out=out_ap, in_=in_ap, func=func, **kw), reads=reads, writes=writes)

        def tt(out_ap, in0, in1, op, reads, writes):
            S.op("dve", lambda e: e.tensor_tensor(out=out_ap, in0=in0, in1=in1, op=op), reads=reads, writes=writes)

        def ts(out_ap, in0, s1, s2, op0, op1, reads, writes):
            S.op("dve", lambda e: e.tensor_scalar(out=out_ap, in0=in0, scalar1=s1, scalar2=s2, op0=op0, op1=op1),
                 reads=reads, writes=writes)

        def stt(out_ap, in0, scalar, in1, op0, op1, reads, writes):
            S.op("dve", lambda e: e.scalar_tensor_tensor(out=out_ap, in0=in0, scalar=scalar, in1=in1, op0=op0, op1=op1),
                 reads=reads, writes=writes)

        S.dma("sp", "cst", lambda e: e.dma_start(out=cst_t[:], in_=cst[:, :]), writes=[cstb])
        S.dma("sp", "cols", lambda e: e.dma_start(out=cols_t[:], in_=cols[:, :]), writes=[colsb])
        S.op("dve", lambda e: e.memset(tmp_t[5][:, 0:P], 1.0), writes=[tmp[5]])
        S.op("dve", lambda e: e.tensor_copy(out=onesr_t[:], in_=tmp_t[5][:, 0:P]), reads=[tmp[5]], writes=[ones])
        S.op("dve", lambda e: e.tensor_copy(out=idb_t[:], in_=IDF), reads=[cstb], writes=[idb])
        act(lbc_t[:, 0:8], cols_t[:, 80:88], AF.Sigmoid, [colsb], [lbc])
        ts(lbc_t[:, 8:16], lbc_t[:, 0:8], -1.0, 1.0, ALU.mult, ALU.add, [lbc], [lbc])
        ts(lbc_t[:, 16:24], lbc_t[:, 0:8], 1.0, -1.0, ALU.mult, ALU.add, [lbc], [lbc])
        for h in range(4):
            S.op("dve", lambda e, h=h: e.memset(R_t[:, h, :], 0.0), writes=[Rf[h]])
            S.op("dve", lambda e, h=h: e.memset(Rb_t[:, h, :], 0.0), writes=[Rb[h]])
        for h in range(8):
            S.op("dve", lambda e, h=h: e.memset(Sx_t[:, h, :], 0.0), writes=[Sf[h]])
            S.op("dve", lambda e, h=h: e.memset(Sb_t[:, h, :], 0.0), writes=[Sb[h]])

        ones_r = onesr_t[:]
        xA_ap = mT_t[:, 0:8, :].rearrange("p a b -> p (a b)").bitcast(F32)
        xB_ap = mT_t[:, 8:16, :].rearrange("p a b -> p (a b)").bitcast(F32)
        XB = [(xin_t[:], [xin]), (xA_ap, [xinA] + mT[0:8]), (xB_ap, [xinB] + mT[8:16])]

        def load_x(blk):
            for i in range(NT):
                load_x_tile(blk, i)
            load_x_stats()
            norm_finish(0)

        def load_x_stats():
            for c in range(DC):
                norm_sq(c)
                if c >= 1:
                    norm_mm(c - 1)
            norm_mm(DC - 1)

        def load_x_tile(blk, i):
            if True:
                r0 = (blk * NT + i) * P
                xap, xbufs = XB[rot["x"] % 3]
                rot["x"] += 1
                tl = slice(i * P, (i + 1) * P)
                S.dma("sp", "xld" + str(rot["x"] % 3), lambda e, r0=r0, xap=xap: e.dma_start(out=xap, in_=x_all[r0:r0 + P, :]), writes=xbufs)
                for g in range(4):
                    pm = nextM()

                    def fn(e, pm=pm, g=g, xap=xap):
                        ins = None
                        for k in range(4):
                            c = 4 * g + k
                            ins = e.transpose(out=pm.t[:, k * P:(k + 1) * P], in_=xap[:, c * P:(c + 1) * P], identity=IDF)
                        return ins
                    S.op("pe", fn, reads=xbufs + [cstb], writes=[pm])
                    o_ap = xT_t[:, 4 * g:4 * g + 4, tl]
                    i_ap = pm.t[:].rearrange("p (k n) -> p k n", n=P)
                    if g % 2 == 0:
                        S.op("act", lambda e, o_ap=o_ap, i_ap=i_ap: e.copy(out=o_ap, in_=i_ap), reads=[pm], writes=xT[4 * g:4 * g + 4])
                    else:
                        S.op("dve", lambda e, o_ap=o_ap, i_ap=i_ap: e.tensor_copy(out=o_ap, in_=i_ap), reads=[pm], writes=xT[4 * g:4 * g + 4])

        def norm_sq(c):
            s_ = sq[c % 2]
            act(s_.t[:], xT_t[:, c, :], AF.Square, [xT[c]], [s_])

        def norm_mm(c):
            s_ = sq[c % 2]
            mm_group(PST.t[:], [(ones_r, s_.t[:])], [ones, s_], [PST], first=(c == 0), last=(c == DC - 1))

        def norm_finish(k, final=False):
            act(rs_t[:], PST.t[:], AF.Sqrt, [PST], [rs], scale=1.0 / D, bias=EPS)
            S.op("dve", lambda e: e.reciprocal(out=rs_t[:], in_=rs_t[:]), reads=[rs], writes=[rs])
            for c in range(DC):
                if final:
                    stt(xT_t[:, c, :], xT_t[:, c, :], GCOL(k, c), rs_t[:], ALU.mult, ALU.mult, [xT[c], colsb, rs], [xT[c]])
                else:
                    stt(hT_t[:, c, :], xT_t[:, c, :], GCOL(k, c), rs_t[:], ALU.mult, ALU.mult, [xT[c], colsb, rs], [hT[c]])

        def ffn(which):
            for half in range(2):
                for fl in range(FH):
                    f = half * FH + fl
                    slot = load_w(wgu[which][f], 4096, 2048)
                    pg = nextA()
                    pu = nextA()
                    if f == 0:
                        pr = [[slot, hT[c]] for c in range(DC)]
                        mm_group(pg.t[:], [(slot.t[:, c * P:(c + 1) * P], hT_t[:, c, :]) for c in range(DC)], [slot] + hT, [pg], pair_reads=pr)
                        mm_group(pu.t[:], [(slot.t[:, 2048 + c * P:2048 + (c + 1) * P], hT_t[:, c, :]) for c in range(DC)], [slot] + hT, [pu])
                    else:
                        def fn(e, slot=slot, pg=pg, pu=pu):
                            ins = None
                            for (pp, off) in ((pg, 0), (pu, 2048)):
                                for c in range(DC):
                                    ins = e.matmul(pp.t[:], lhsT=slot.t[:, off + c * P:off + (c + 1) * P], rhs=hT_t[:, c, :],
                                                   start=(c == 0), stop=(c == DC - 1))
                            return ins
                        S.op("pe", fn, reads=[slot] + hT, writes=[pg, pu])
                    s = sg[fl % 2]
                    act(s.t[:], pg.t[:], AF.Silu, [pg], [s])
                    tt(a_t[:, fl, :], s.t[:], pu.t[:], ALU.mult, [s, pu], [a[fl]])
                for c in range(DC):
                    slot = load_w(wdn[which][half * DC + c], FH * P, 1408)
                    py = nextA()
                    mm_group(py.t[:], [(slot.t[:, fl * P:(fl + 1) * P], a_t[:, fl, :]) for fl in range(FH)], [slot] + a, [py])
                    stt(xT_t[:, c, :], py.t[:], 0.5, xT_t[:, c, :], ALU.mult, ALU.add, [py, xT[c]], [xT[c]])
                    if half == 1:
                        norm_sq(c)
                        if c >= 1:
                            norm_mm(c - 1)
                if half == 1:
                    norm_mm(DC - 1)

        def tm_proj(u, half, dst_t, dst, func):
            slot = load_w(wtm[2 * u + half], 4096, 2048)
            for i in range(NT):
                pa = nextA()
                o_ap = pa.t[:, 0:256]
                mm_group(o_ap, [(hT_t[:, c, i * P:(i + 1) * P], slot.t[:, c * 256:(c + 1) * 256]) for c in range(DC)],
                         [slot] + hT, [pa])
                if func is None:
                    S.op("act", lambda e, o_ap=o_ap, i=i: e.copy(out=dst_t[:, i, :], in_=o_ap), reads=[pa], writes=[dst[i]])
                else:
                    act(dst_t[:, i, :], o_ap, func, [pa], [dst[i]])

        def fm_proj(slot, k, split=False):
            pa = nextA()
            pr = [[slot, hT[c]] for c in range(DC)] if split else None
            mm_group(pa.t[:], [(slot.t[:, (k * DC + c) * P:(k * DC + c + 1) * P], hT_t[:, c, :]) for c in range(DC)], [slot] + hT, [pa],
                     pair_reads=pr)
            return pa

        def rope(slot, dst_t, dst, split=False):
            p1 = fm_proj(slot, 0, split)
            p2 = fm_proj(slot, 1)
            COS = cs_t[:, 0:T]
            SIN = cs_t[:, T:2 * T]
            tt(tmp_t[0][:], p1.t[:], COS, ALU.mult, [p1, cs], [tmp[0]])
            tt(tmp_t[3][:], p1.t[:], SIN, ALU.mult, [p1, cs], [tmp[3]])
            tt(tmp_t[1][:], p2.t[:], SIN, ALU.mult, [p2, cs], [tmp[1]])
            tt(tmp_t[2][:], p2.t[:], COS, ALU.mult, [p2, cs], [tmp[2]])
            tt(dst_t[:, 0, :], tmp_t[0][:], tmp_t[1][:], ALU.subtract, [tmp[0], tmp[1]], [dst])
            tt(dst_t[:, 1, :], tmp_t[2][:], tmp_t[3][:], ALU.add, [tmp[2], tmp[3]], [dst])

        def ret_unit(h, full, need_bf=True):
            if full:
                sq_ = load_w(wfm[2 * h], 4096, 2048)
                rope(sq_, qT_t, qT, split=(h == 0))
                for cc in range(2):
                    for i in range(NT):
                        tt(qd_t[:, cc, i * P:(i + 1) * P], qT_t[:, cc, i * P:(i + 1) * P], QDEC(h), ALU.mult, [qT, cstb], [qd])
            sk = load_w(wfm[2 * h + 1], 4096, 2048)
            rope(sk, kT_t, kT, split=(h == 0 and not full))
            tm_proj(h, 0, v_t, vb, None)
            if full:
                tm_proj(h, 1, gt_t, gb, AF.Silu)
            def fn(e):
                ins = None
                for i in range(NT):
                    for cc in range(2):
                        ins = e.transpose(out=psTb_t[:, (2 * i + cc) * P:(2 * i + cc + 1) * P], in_=kT_t[:, cc, i * P:(i + 1) * P],
                                          identity=idb_t[:])
                return ins
            S.op("pe", fn, reads=[kT, idb], writes=[PTb])
            ts(kd_t[:].rearrange("p a b -> p (a b)"), psTb_t[:], KDC(h), None, ALU.mult, ALU.bypass, [PTb, cstb], kd)
            if full:
                pm = nextM()
                for i in range(NT):
                    tsl = slice(i * P, (i + 1) * P)
                    mm_group(pm.t[:, tsl], [(kT_t[:, cc, tsl], qT_t[:, cc, tsl]) for cc in range(2)], [kT, qT], [pm])
                for i in range(NT):
                    tsl = slice(i * P, (i + 1) * P)
                    tt(pt_t[:, tsl], pm.t[:, tsl], MASKT(h), ALU.mult, [pm, cstb], [pt])
            for i in range(NT):
                tsl = slice(i * P, (i + 1) * P)
                if full:
                    po = nextM()
                    mm_group(po.t[:, 0:256], [(pt_t[:, tsl], v_t[:, i, :])], [pt, vb[i]], [po], first=True, last=False)
                pk = nextA()
                for cc in range(2):
                    mm_group(pk.t[:, cc * 256:(cc + 1) * 256], [(kd_t[:, i, cc * P:(cc + 1) * P], v_t[:, i, :])], [kd[i], vb[i]], [pk])
                if full:
                    mm_group(po.t[:, 0:256], [(qd_t[:, cc, tsl], Rb_t[:, h, cc * 256:(cc + 1) * 256]) for cc in range(2)],
                             [qd, Rb[h]], [po], first=False, last=True)
                    S.op("act", lambda e, po=po: e.copy(out=o_t[:], in_=po.t[:, 0:256]), reads=[po], writes=[ob])
                    stt(Rb_t[:, h, :], R_t[:, h, :], GAM[h] ** 128, pk.t[:], ALU.mult, ALU.add, [Rf[h], pk], [Rb[h]])
                stt(R_t[:, h, :], R_t[:, h, :], GAM[h] ** 128, pk.t[:], ALU.mult, ALU.add, [Rf[h], pk], [Rf[h]])
                if need_bf and (not full) and i == NT - 1:
                    S.op("act", lambda e, h=h: e.copy(out=Rb_t[:, h, :], in_=R_t[:, h, :]), reads=[Rf[h]], writes=[Rb[h]])
                if full:
                    S.op("dve", lambda e: e.bn_stats(out=st_t[:, 0:6], in_=o_t[:]), reads=[ob], writes=[stb])
                    S.op("dve", lambda e: e.bn_aggr(out=st_t[:, 8:10], in_=st_t[:, 0:6]), reads=[stb], writes=[stb])
                    act(st_t[:, 10:11], st_t[:, 9:10], AF.Sqrt, [stb], [stb], bias=EPS)
                    S.op("dve", lambda e: e.reciprocal(out=st_t[:, 10:11], in_=st_t[:, 10:11]), reads=[stb], writes=[stb])
                    ts(y_t[:], o_t[:], st_t[:, 8:9], st_t[:, 10:11], ALU.subtract, ALU.mult, [ob, stb], [yb])
                    tt(mrg_t[:, i, h * 256:(h + 1) * 256], y_t[:], gt_t[:, i, :], ALU.mult, [yb, gb[i]], [mrg[i]])

        def hgrn_unit(hp, full, need_bf=True):
            u = 4 + hp
            sf = load_w(wfm[2 * u + 1], 4096, 2048)
            if full:
                sq_ = load_w(wfm[2 * u], 4096, 2048)
            for hh in range(2):
                hd = 2 * hp + hh
                pz = fm_proj(sf, hh)
                act(tmp_t[0][:], pz.t[:], AF.Sigmoid, [pz], [tmp[0]])
                ts(tmp_t[1][:], tmp_t[0][:], OML(hd), LB(hd), ALU.mult, ALU.add, [tmp[0], lbc], [tmp[1]])
                act(tmp_t[1][:], tmp_t[1][:], AF.Ln, [tmp[1]], [tmp[1]])
                S.op("dve", lambda e: e.tensor_tensor_scan(out=tmp_t[2][:], data0=RESET, data1=tmp_t[1][:], initial=0.0,
                                                           op0=ALU.mult, op1=ALU.add), reads=[tmp[1], cstb], writes=[tmp[2]])
                act(tmp_t[3][:], tmp_t[2][:], AF.Exp, [tmp[2]], [tmp[3]], scale=-1.0)
                ts(tmp_t[4][:], tmp_t[0][:], NOML(hd), OML(hd), ALU.mult, ALU.add, [tmp[0], lbc], [tmp[4]])
                tt(kT_t[:, hh, :], tmp_t[4][:], tmp_t[3][:], ALU.mult, [tmp[4], tmp[3]], [kT])
                act(tmp_t[5][:], tmp_t[2][:], AF.Exp, [tmp[2]], [tmp[5]])
                ecv = tmp_t[5][:].rearrange("p (c j) -> p c j", j=64)[:, :, 63:64]
                S.op("dve", lambda e, hh=hh, ecv=ecv: e.tensor_copy(out=el_t[:, hh, :].rearrange("p (c o) -> p c o", o=1), in_=ecv),
                     reads=[tmp[5]], writes=[el])
                if full:
                    pq = fm_proj(sq_, hh)
                    act(tmp_t[0][:], pq.t[:], AF.Silu, [pq], [tmp[0]])
                    tt(qT_t[:, hh, :], tmp_t[0][:], tmp_t[5][:], ALU.mult, [tmp[0], tmp[5]], [qT])
            tm_proj(u, 0, v_t, vb, None)
            if full:
                tm_proj(u, 1, gt_t, gb, AF.Silu)
            def fn(e):
                ins = None
                for i in range(NT):
                    for hh in range(2):
                        ins = e.transpose(out=psTb_t[:, (2 * i + hh) * P:(2 * i + hh + 1) * P], in_=kT_t[:, hh, i * P:(i + 1) * P],
                                          identity=idb_t[:])
                return ins
            S.op("pe", fn, reads=[kT, idb], writes=[PTb])
            S.op("act", lambda e: e.copy(out=kd_t[:].rearrange("p a b -> p (a b)"), in_=psTb_t[:]), reads=[PTb], writes=kd)
            ptq = qd_t[:].rearrange("p a b -> p (a b)")
            if full:
                pms = [nextA(), nextA()]
                for i in range(NT):
                    tsl = slice(i * P, (i + 1) * P)
                    for hh in range(2):
                        col = ((i % 2) * 2 + hh) * P
                        mm_group(pms[i // 2].t[:, col:col + P], [(kT_t[:, hh, tsl], qT_t[:, hh, tsl])], [kT, qT], [pms[i // 2]])
                for i in range(NT):
                    for hh in range(2):
                        col = ((i % 2) * 2 + hh) * P
                        tt(ptq[:, (i * 2 + hh) * P:(i * 2 + hh + 1) * P], pms[i // 2].t[:, col:col + P], BDM, ALU.mult, [pms[i // 2], cstb], [qd])
            for i in range(NT):
                tsl = slice(i * P, (i + 1) * P)
                po = [None, None]
                if full:
                    po = [nextA(), nextA()]
                if full:
                    for c2 in range(2):
                        for hh in range(2):
                            hs = slice(hh * P, (hh + 1) * P)
                            mm_group(po[hh].t[c2 * 64:(c2 + 1) * 64, 0:P],
                                     [(ptq[:, (i * 2 + hh) * P + c2 * 64:(i * 2 + hh) * P + (c2 + 1) * 64], v_t[:, i, hs])],
                                     [qd, vb[i]], [po[hh]], first=True, last=False)
                for c2 in range(2):
                    pk = nextA()
                    pko = 0
                    for hh in range(2):
                        vs = slice(hh * P, (hh + 1) * P)
                        mm_group(pk.t[:, vs], [(kd_t[c2 * 64:(c2 + 1) * 64, i, vs], v_t[c2 * 64:(c2 + 1) * 64, i, vs])], [kd[i], vb[i]], [pk])
                    if full:
                        for hh in range(2):
                            hd = 2 * hp + hh
                            mm_group(po[hh].t[c2 * 64:(c2 + 1) * 64, 0:P],
                                     [(qT_t[:, hh, i * P + c2 * 64:i * P + (c2 + 1) * 64], Sb_t[:, hd, :])],
                                     [qT, Sb[hd]], [po[hh]], first=False, last=True)
                    h0 = 2 * hp
                    tt(stmp_t[:], pk.t[:, pko:pko + 256], Sx_t[:, h0:h0 + 2, :].rearrange("p a b -> p (a b)"), ALU.add, [pk, Sf[h0], Sf[h0 + 1]], [stmp])
                    for hh in range(2):
                        hd = 2 * hp + hh
                        hs = slice(hh * P, (hh + 1) * P)
                        ecol = el_t[:, hh, 2 * i + c2:2 * i + c2 + 1]
                        if full:
                            ts(Sb_t[:, hd, :], stmp_t[:, hs], ecol, None, ALU.mult, ALU.bypass, [stmp, el], [Sb[hd]])
                        ts(Sx_t[:, hd, :], stmp_t[:, hs], ecol, None, ALU.mult, ALU.bypass, [stmp, el], [Sf[hd]])
                        if need_bf and (not full) and i == NT - 1 and c2 == 1:
                            S.op("act", lambda e, hd=hd: e.copy(out=Sb_t[:, hd, :], in_=Sx_t[:, hd, :]), reads=[Sf[hd]], writes=[Sb[hd]])
                if full:
                    for hh in range(2):
                        hs = slice(hh * P, (hh + 1) * P)
                        S.op("act", lambda e, hh=hh, hs=hs, p_=po[hh]: e.copy(out=o_t[:, hs], in_=p_.t[:, 0:P]), reads=[po[hh]], writes=[ob])
                        S.op("act", lambda e, hh=hh, hs=hs: e.activation(out=y_t[:, hs], in_=o_t[:, hs], func=AF.Square,
                                                                        accum_out=st_t[:, hh:hh + 1]), reads=[ob], writes=[yb, stb])
                    act(st_t[:, 2:4], st_t[:, 0:2], AF.Sqrt, [stb], [stb], scale=1.0 / P, bias=EPS)
                    S.op("dve", lambda e: e.reciprocal(out=st_t[:, 2:4], in_=st_t[:, 2:4]), reads=[stb], writes=[stb])
                    for hh in range(2):
                        hs = slice(hh * P, (hh + 1) * P)
                        stt(mrg_t[:, i, u * 256 + hh * P:u * 256 + (hh + 1) * P], o_t[:, hs], st_t[:, 2 + hh:3 + hh], gt_t[:, i, hs],
                            ALU.mult, ALU.mult, [ob, stb, gb[i]], [mrg[i]])

        def mixer(blk, full, prefetch=False):
            norm_finish(1)
            S.dma("sp", "csl", lambda e, blk=blk: e.dma_start(out=cs_t[:], in_=cs_all[blk]), writes=[cs])
            if DBG >= 4:
                for h in range(4):
                    ret_unit(h, full, full or blk == N_PRE - 1)
                    if prefetch:
                        load_x_tile(blk + 1, h)
            if DBG >= 5:
                for hp in range(4):
                    hgrn_unit(hp, full, full or blk == N_PRE - 1)
                    if prefetch and hp == 0:
                        load_x_stats()
                if prefetch:
                    norm_finish(0)
            if not full or DBG < 6:
                return
            for i in range(NT):
                for half in range(2):
                    def fn(e, i=i, half=half):
                        ins = None
                        for k in range(8):
                            m = 8 * half + k
                            ins = e.transpose(out=psTb_t[:, k * P:(k + 1) * P], in_=mrg_t[:, i, m * P:(m + 1) * P], identity=idb_t[:])
                        return ins
                    S.op("pe", fn, reads=[mrg[i], idb], writes=[PTb])
                    for k in range(8):
                        m = 8 * half + k
                        if True:
                            ts(mT_t[:, m, i * P:(i + 1) * P], psTb_t[:, k * P:(k + 1) * P], GCOL(4, m), None, ALU.mult, ALU.bypass,
                               [PTb, colsb], [mT[m]])
                        else:
                            S.op("act", lambda e, m=m, k=k, i=i: e.activation(out=mT_t[:, m, i * P:(i + 1) * P], in_=psTb_t[:, k * P:(k + 1) * P],
                                                                              func=AF.Copy, scale=GCOL(4, m)),
                                 reads=[PTb, colsb], writes=[mT[m]])
            for g in range(8):
                slot = load_w(wout[g], 4096, 2048)
                for k in range(2):
                    c = 2 * g + k
                    pa = nextA()
                    mm_group(pa.t[:], [(slot.t[:, (k * DC + m) * P:(k * DC + m + 1) * P], mT_t[:, m, :]) for m in range(DC)], [slot] + mT, [pa])
                    tt(xT_t[:, c, :], xT_t[:, c, :], pa.t[:], ALU.add, [xT[c], pa], [xT[c]])
                    norm_sq(c)
                    if c >= 1:
                        norm_mm(c - 1)
            norm_mm(DC - 1)

        def store_out(ob_i):
            norm_finish(3, final=True)
            for i in range(NT):
                xi = rot["x"] % 3
                xap, xbufs = XB[xi]
                rot["x"] += 1
                for g in range(4):
                    pm = nextM()

                    def fn(e, pm=pm, g=g, i=i):
                        ins = None
                        for k in range(4):
                            c = 4 * g + k
                            ins = e.transpose(out=pm.t[:, k * P:(k + 1) * P], in_=xT_t[:, c, i * P:(i + 1) * P], identity=IDF)
                        return ins
                    S.op("pe", fn, reads=xT[4 * g:4 * g + 4] + [cstb], writes=[pm])
                    if g % 2 == 0:
                        S.op("act", lambda e, pm=pm, g=g, xap=xap: e.copy(out=xap[:, g * T:(g + 1) * T], in_=pm.t[:]), reads=[pm], writes=xbufs)
                    else:
                        S.op("dve", lambda e, pm=pm, g=g, xap=xap: e.tensor_copy(out=xap[:, g * T:(g + 1) * T], in_=pm.t[:]), reads=[pm], writes=xbufs)
                r0 = (ob_i * NT + i) * P
                S.dma("sp", "xst" + str(xi), lambda e, r0=r0, xap=xap: e.dma_start(out=out[r0:r0 + P, :], in_=xap), reads=xbufs)

        preloaded = False
        for blk in range(NB):
            full = blk >= N_PRE
            if not preloaded:
                load_x(blk)
            if DBG >= 2:
                ffn(0)
            preloaded = (not full) and blk + 1 < NB
            if DBG >= 3:
                mixer(blk, full, prefetch=preloaded)
            if full:
                if DBG >= 7:
                    norm_finish(2)
                    ffn(1)
                store_out(blk - N_PRE)
        S.wait("sp", [(k, v) for k, v in S.dcount.items() if k.startswith("xst")])

        with nc.Block() as block:
            def replay(name):
                def body(e):
                    for waits, fn, inc in S.streams[name]:
                        for (s, v) in waits:
                            e.wait_ge(sem(s), v)
                        if fn is not None:
                            ins = fn(e)
                            ins.then_inc(sem(inc[0]), inc[1])
                return body
            block.tensor(replay("pe"))
            block.scalar(replay("act"))
            block.vector(replay("dve"))
            block.gpsimd(replay("pool"))
            block.sync(replay("sp"))
    return nc


def _fm(W, starts):
    parts = [W[:, s:s + P].reshape(DC, P, P).transpose(1, 0, 2) for s in starts]
    return np.ascontiguousarray(np.stack(parts, axis=1).reshape(P, -1))


def _tm(W, s0):
    return np.ascontiguousarray(W[:, s0:s0 + 256].reshape(DC, P, 256).transpose(1, 0, 2).reshape(P, -1))


def _prep_weights(inp):
    f32 = np.float32
    w = {}
    for idx, pre in enumerate(["ffn1", "ffn2"]):
        Wg = np.asarray(inp[pre + "_w_gate"][0], f32)
        Wu = np.asarray(inp[pre + "_w_up"][0], f32)
        Wd = np.asarray(inp[pre + "_w_down"][0], f32)
        g = Wg.reshape(DC, P, FC, P).transpose(2, 1, 0, 3)
        u = Wu.reshape(DC, P, FC, P).transpose(2, 1, 0, 3)
        w[f"wgu{idx + 1}"] = np.ascontiguousarray(np.stack([g, u], axis=2).reshape(FC, P, 4096))
        d = Wd.reshape(2, FH, P, DC, P).transpose(0, 3, 2, 1, 4)
        w[f"wd{idx + 1}"] = np.ascontiguousarray(d.reshape(2 * DC, P, FH * P))
    Win = np.asarray(inp["w_in"][0], f32)
    fm, tm = [], []
    for u_ in range(8):
        if u_ < 4:
            qb, kb, vbs, gbs = 0 + u_ * 256, 1024 + u_ * 256, 2048 + u_ * 256, 3072 + u_ * 256
        else:
            hp = u_ - 4
            qb, kb, vbs, gbs = 4096 + hp * 256, 5120 + hp * 256, 6144 + hp * 256, 7168 + hp * 256
        fm.append(_fm(Win, [qb, qb + P]))
        fm.append(_fm(Win, [kb, kb + P]))
        tm.append(_tm(Win, vbs))
        tm.append(_tm(Win, gbs))
    w["wfm"] = np.stack(fm)
    w["wtm"] = np.stack(tm)
    Wo = np.asarray(inp["w_out"][0], f32)
    w["wout"] = np.stack([_fm(Wo, [2 * g * P, (2 * g + 1) * P]) for g in range(8)])
    return w


def _consts():
    c = np.zeros((P, 1800), np.float32)
    j = np.arange(P)[:, None].astype(np.float64)
    i = np.arange(P)[None, :].astype(np.float64)
    for h in range(4):
        g = GAM[h]
        m = np.where(i >= j, g ** np.maximum(i - j, 0.0), 0.0) / 16.0
        c[:, h * P:(h + 1) * P] = m
        row = (g ** (np.arange(P, dtype=np.float64) + 1.0)) / 16.0
        c[:, 512 + h * 128:512 + (h + 1) * 128] = row[None, :]
        c[:, 1792 + h] = g ** (127.0 - np.arange(P, dtype=np.float64))
    same = (np.arange(P)[:, None] // 64) == (np.arange(P)[None, :] // 64)
    c[:, 1024:1152] = np.where(same & (i >= j), 1.0, 0.0)
    c[:, 1152:1664] = np.where(np.arange(T) % 64 == 0, 0.0, 1.0)[None, :]
    c[:, 1664:1792] = np.eye(P)
    return c


def _rope_tables(pos0, NB=NB):
    inv = np.power(np.float32(10000.0), -(np.arange(0, 256, 2, dtype=np.float32) / np.float32(256.0))).astype(np.float32)
    tabs = np.zeros((NB, P, 2 * T), np.float32)
    for b in range(NB):
        pos = (pos0 + b * T + np.arange(T)).astype(np.float32)
        ang = (pos[None, :] * inv[:, None]).astype(np.float32).astype(np.float64)
        tabs[b, :, 0:T] = np.cos(ang)
        tabs[b, :, T:] = np.sin(ang)
    return tabs


_NC_CACHE = {}


def kernel(**inp):
    x = np.asarray(inp["x"], np.float32)
    w = _prep_weights(inp)
    cst = _consts()
    cols = np.zeros((P, 128), np.float32)
    for k, name in enumerate(["ffn1_norm", "mix_norm", "ffn2_norm", "final_norm"]):
        cols[:, k * 16:(k + 1) * 16] = np.asarray(inp[name], np.float32).reshape(DC, P).T
    gm = np.concatenate([np.asarray(inp["ret_norm_g"], np.float32).reshape(-1), np.asarray(inp["hgrn_norm_g"], np.float32).reshape(-1)])
    cols[:, 64:80] = gm.reshape(DC, P).T
    cols[:, 80:88] = np.asarray(inp["hgrn_lb_logits"], np.float32).reshape(8, P).T
    in_maps = []
    for core in range(8):
        b, j = core // 4, core % 4
        xa = np.zeros((NB * T, D), np.float32)
        lo = j * SEG - N_PRE * T
        src_lo = max(lo, 0)
        xa[src_lo - lo:, :] = x[b, src_lo:(j + 1) * SEG, :]
        m = dict(w)
        m["x_all"] = xa
        m["cs_all"] = _rope_tables(lo)
        m["cst"] = cst
        m["cols"] = cols
        in_maps.append(m)
    if "nc" not in _NC_CACHE:
        _NC_CACHE["nc"] = build_nc()
    res = run_bass_kernel_spmd(_NC_CACHE["nc"], in_maps, core_ids=list(range(8)))
    outp = np.zeros((2, SEQ, D), np.float32)
    for core in range(8):
        b, j = core // 4, core % 4
        outp[b, j * SEG:(j + 1) * SEG, :] = res.results[core]["out"]
    return outp
```
